# Optimizing a Trainium2 kernel written in Bass

```python
import jax, jax.numpy as jnp
from jax import lax
import numpy as np

D_MODEL = 1024
BATCH = 4
SEQ = 4096
DEPTH = 1

C_CONV = D_MODEL
CONV_WIDTH = 31
POOL_WINDOWS = (2, 4, 8, 16)
N_POOL_GROUPS = len(POOL_WINDOWS)
C_POOL = D_MODEL
POOL_GROUP = C_POOL // N_POOL_GROUPS
N_BRANCHES = 2
D_FF = 4 * D_MODEL
IN_COLS = 2 * C_CONV + C_POOL + N_BRANCHES * D_MODEL
RMS_EPS = 1e-6
LN_EPS = 1e-5

kernel_name = "hybrid_conv_pool_gated_block"


def rms_norm(x, g):
    xf = x.astype(jnp.float32)
    y = xf * lax.rsqrt(jnp.mean(xf * xf, axis=-1, keepdims=True) + RMS_EPS)
    return (y * g.astype(jnp.float32)).astype(x.dtype)


def layer_norm(x, g, b):
    xf = x.astype(jnp.float32)
    mu = jnp.mean(xf, axis=-1, keepdims=True)
    var = jnp.mean(jnp.square(xf - mu), axis=-1, keepdims=True)
    y = (xf - mu) * lax.rsqrt(var + LN_EPS)
    return (y * g.astype(jnp.float32) + b.astype(jnp.float32)).astype(x.dtype)


def depthwise_causal_conv(u, k, b):
    out = lax.conv_general_dilated(
        u, k[:, None, :].astype(u.dtype), window_strides=(1,),
        padding=[(CONV_WIDTH - 1, 0)],
        dimension_numbers=("NWC", "WIO", "NWC"),
        feature_group_count=u.shape[-1])
    return out + b.astype(u.dtype)


def conformer_conv_branch(u_glu, dw_kernel, dw_bias, ln_g, ln_b, w_out):
    a, gate = jnp.split(u_glu, 2, axis=-1)
    u = a * jax.nn.sigmoid(gate)
    u = depthwise_causal_conv(u, dw_kernel, dw_bias)
    u = layer_norm(u, ln_g, ln_b)
    u = jax.nn.swish(u)
    return u @ w_out


def causal_multiscale_pool(p):
    B, S, _ = p.shape
    pg = p.reshape(B, S, N_POOL_GROUPS, POOL_GROUP)
    cs = jnp.cumsum(pg.astype(jnp.float32), axis=1)
    pos = jnp.arange(1, S + 1, dtype=jnp.int32)
    outs = []
    for g, w in enumerate(POOL_WINDOWS):
        c = cs[:, :, g]
        prev = jnp.pad(c[:, :-w], ((0, 0), (w, 0), (0, 0)))
        cnt = jnp.minimum(pos, w).astype(jnp.float32)[None, :, None]
        outs.append((c - prev) / cnt)
    pooled = jnp.stack(outs, axis=2) - pg.astype(jnp.float32)
    return pooled.astype(p.dtype)


def pooling_branch(p, pool_w, pool_scale, w_out):
    B, S, _ = p.shape
    z = causal_multiscale_pool(p)
    z = jnp.einsum("bsgc,gcd->bsgd", z, pool_w).reshape(B, S, C_POOL)
    z = z * pool_scale
    return z @ w_out


def setup_inputs(seed: int = 0) -> dict:
    key = jax.random.key(seed)
    ks = jax.random.split(key, 20)
    f32 = jnp.float32
    nrm = lambda k, shape, s: jax.random.normal(k, shape, f32) * s
    return {
        "x": jax.random.normal(ks[0], (BATCH, SEQ, D_MODEL), f32),
        "mix_pre_g": 1.0 + nrm(ks[1], (D_MODEL,), 0.05),
        "w_in": nrm(ks[2], (D_MODEL, IN_COLS), D_MODEL ** -0.5),
        "dw_kernel": nrm(ks[3], (CONV_WIDTH, C_CONV), CONV_WIDTH ** -0.5),
        "dw_bias": nrm(ks[4], (C_CONV,), 0.02),
        "conv_ln_g": 1.0 + nrm(ks[5], (C_CONV,), 0.05),
        "conv_ln_b": nrm(ks[6], (C_CONV,), 0.02),
        "w_conv_out": nrm(ks[7], (C_CONV, D_MODEL), C_CONV ** -0.5),
        "pool_w": nrm(ks[8], (N_POOL_GROUPS, POOL_GROUP, POOL_GROUP), POOL_GROUP ** -0.5),
        "pool_scale": 1.0 + nrm(ks[9], (C_POOL,), 0.1),
        "w_pool_out": nrm(ks[10], (C_POOL, D_MODEL), C_POOL ** -0.5),
        "w_o": nrm(ks[11], (D_MODEL, D_MODEL), D_MODEL ** -0.5),
        "mix_post_g": 1.0 + nrm(ks[12], (D_MODEL,), 0.05),
        "mlp_pre_g": 1.0 + nrm(ks[13], (D_MODEL,), 0.05),
        "w_ff1": nrm(ks[14], (D_MODEL, D_FF), D_MODEL ** -0.5),
        "w_ff2": nrm(ks[15], (D_FF, D_MODEL), D_FF ** -0.5),
        "mlp_post_g": 1.0 + nrm(ks[16], (D_MODEL,), 0.05),
    }


def reference(x, mix_pre_g, w_in, dw_kernel, dw_bias, conv_ln_g, conv_ln_b,
              w_conv_out, pool_w, pool_scale, w_pool_out, w_o, mix_post_g,
              mlp_pre_g, w_ff1, w_ff2, mlp_post_g):
    h = x
    for _ in range(DEPTH):
        u = rms_norm(h, mix_pre_g)
        proj = u @ w_in
        u_glu = proj[..., :2 * C_CONV]
        p = proj[..., 2 * C_CONV:2 * C_CONV + C_POOL]
        gates = jax.nn.sigmoid(proj[..., 2 * C_CONV + C_POOL:])
        g_conv, g_pool = jnp.split(gates, N_BRANCHES, axis=-1)
        y_conv = conformer_conv_branch(u_glu, dw_kernel, dw_bias, conv_ln_g, conv_ln_b, w_conv_out)
        y_pool = pooling_branch(p, pool_w, pool_scale, w_pool_out)
        merged = g_conv * y_conv + g_pool * y_pool
        h = h + rms_norm(merged @ w_o, mix_post_g)
        v = rms_norm(h, mlp_pre_g)
        v = jnp.square(jax.nn.relu(v @ w_ff1)) @ w_ff2
        h = h + rms_norm(v, mlp_post_g)
    return h
```

```python
import numpy as np
from contextlib import ExitStack
import concourse.bass as bass
import concourse.mybir as mybir
from concourse.bass_utils import run_bass_kernel_spmd

F32 = mybir.dt.float32
BF16 = mybir.dt.bfloat16
AF = mybir.ActivationFunctionType
ALU = mybir.AluOpType
AX = mybir.AxisListType

H = 32
NT = 2048
TT = H + NT
DM = 1024
NCORES = 8
RMS_EPS = 1e-6
LN_EPS = 1e-5
POOL_W = (2, 4, 8, 16)

OFF_RA, RA_N = 0, 8 * TT
OFF_RB, RB_N = OFF_RA + RA_N, 8 * TT
OFF_RD, RD_N = OFF_RB + RB_N, 8 * NT
OFF_RW, RW_N = OFF_RD + RD_N, 8 * 2048
OFF_RC, RC_N = OFF_RW + RW_N, 8192
OFF_RDM, RDM_N = OFF_RC + RC_N, 8192
ARENA_N = OFF_RDM + RDM_N


class KB:
    ENG = ("pe", "act", "dve", "pool", "sp")
    N_DMA_SEM = 16

    def __init__(self, nc):
        self.nc = nc
        self.q = {e: [] for e in self.ENG}
        self.count = {e: 0 for e in self.ENG}
        self.last_write = {}
        self.readers = {}
        self.waited = {e: {} for e in self.ENG}
        self.dma_val = [0] * self.N_DMA_SEM
        self.dma_next = 0
        self.n_sw = 0
        self.ranges = {}

    def reg(self, name, start, n):
        self.ranges[name] = (start, start + n)

    def _need(self, eng, ev, waits):
        sk, v = ev
        if sk == "pe" and eng == "pe":
            return
        if self.waited[eng].get(sk, 0) >= v:
            return
        self.waited[eng][sk] = v
        waits[sk] = max(waits.get(sk, 0), v)

    def _clobbers(self, w):
        rg = self.ranges.get(w)
        if rg is None:
            return ()
        out = []
        for n2, (s2, e2) in self.ranges.items():
            if n2 != w and s2 < rg[1] and rg[0] < e2 and (n2 in self.last_write or n2 in self.readers):
                out.append(n2)
        return out

    def op(self, eng, fn, reads=(), writes=(), dma=False, after=()):
        waits = {}
        for ev in after:
            self._need(eng, ev, waits)
        writes = list(writes)
        extra = []
        for w in writes:
            extra.extend(self._clobbers(w))
        for r in reads:
            ev = self.last_write.get(r)
            if ev is not None:
                self._need(eng, ev, waits)
            if isinstance(r, tuple) and r[0] == "ps":
                for ev in self.readers.get(r, ()):
                    if ev[0] != eng:
                        self._need(eng, ev, waits)
        for w in writes + extra:
            ev = self.last_write.get(w)
            if ev is not None:
                self._need(eng, ev, waits)
            for ev in self.readers.get(w, ()):
                self._need(eng, ev, waits)
        if dma and eng == "pool":
            ev = (("w", self.n_sw), 16)
            self.n_sw += 1
        elif dma:
            slot = self.dma_next
            self.dma_next = (self.dma_next + 1) % self.N_DMA_SEM
            if self.dma_val[slot] > 0:
                self._need(eng, (("d", slot), self.dma_val[slot]), waits)
            self.dma_val[slot] += 16
            ev = (("d", slot), self.dma_val[slot])
        else:
            self.count[eng] += 1
            ev = (eng, self.count[eng])
        self.q[eng].append((waits, fn, ev, dma))
        for w in writes:
            self.last_write[w] = ev
            self.readers[w] = []
        for w in extra:
            self.last_write.pop(w, None)
            self.readers.pop(w, None)
        for r in reads:
            self.readers.setdefault(r, []).append(ev)
        return ev

    def final_wait(self, eng, resources):
        waits = {}
        for r in resources:
            ev = self.last_write.get(r)
            if ev is not None:
                self._need(eng, ev, waits)
        self.q[eng].append((waits, None, None, False))

    def emit(self, sems, block):
        deco = {"pe": block.tensor, "act": block.scalar, "dve": block.vector,
                "pool": block.gpsimd, "sp": block.sync}
        needed = {e: set() for e in self.ENG}
        for e in self.ENG:
            for waits, fn, ev, dma in self.q[e]:
                for sk, v in waits.items():
                    if sk in needed:
                        needed[sk].add(v)
        rank = {e: {v: i + 1 for i, v in enumerate(sorted(needed[e]))} for e in self.ENG}
        for e in self.ENG:
            items = self.q[e]

            def body(h, items=items):
                for waits, fn, ev, dma in items:
                    for sk, v in waits.items():
                        h.wait_ge(sems[sk], rank[sk][v] if sk in rank else v)
                    if fn is None:
                        continue
                    inst = fn(h)
                    if dma:
                        inst.then_inc(sems[ev[0]], 16)
                    elif ev[1] in rank[ev[0]]:
                        inst.then_inc(sems[ev[0]], 1)

            deco[e](body)


def build_program(dbg=None):
    nc = bass.Bass("TRN2", target_bir_lowering=False)

    def din(name, shape):
        return nc.dram_tensor(name, shape, F32, kind="ExternalInput").ap()

    xh = din("xh", [TT, DM])
    w_in = din("w_in", [DM, 5120])
    w_conv_out = din("w_conv_out", [DM, DM])
    pool_w = din("pool_w", [4, 256, 256])
    w_pool_out = din("w_pool_out", [DM, DM])
    w_o = din("w_o", [DM, DM])
    w_ff1 = din("w_ff1", [DM, 4096])
    w_ff2 = din("w_ff2", [4096, DM])
    vecs = din("vecs", [128, 32 + 248])
    gbd = din("gb", [4, DM])
    identd = din("ident", [128, 128])
    pmd = din("pm", [128, 4 * 288])
    convw = din("convw", [8, 128, 16 * 3 * 128])
    out = nc.dram_tensor("out", [NT, DM], F32, kind="ExternalOutput").ap()
    dbg_cv = nc.dram_tensor("dbg_cv", [128, 8 * NT], BF16, kind="ExternalOutput").ap() if dbg == "cv" else None

    with ExitStack() as es:
        T = lambda name, shape, dt: es.enter_context(nc.sbuf_tensor(name, shape, dt))
        arena = T("arena", [128, ARENA_N], BF16)
        xt = T("xt", [128, 2, DM], F32)
        xb = T("xb", [128, 3, DM], BF16)
        gb = T("gbt", [128, 2, DM], F32)
        tmpA = T("tmpA", [128, 4, 512], F32)
        stat = T("stat", [128, 4, 512], F32)
        sm = T("sm", [128, 64], F32)
        epst = T("epst", [128, 2], F32)
        mhalf = T("mhalf", [128, 1], F32)
        vec = T("vec", [128, 32 + 248], F32)
        identf = T("identf", [128, 128], F32)
        identb = T("identb", [128, 128], BF16)
        onesb = T("onesb", [128, 128], BF16)
        pmb = T("pmb", [128, 4, 288], BF16)
        junkb = T("junkb", [128, DM], BF16)
        ps = es.enter_context(nc.psum_tensor("ps", [128, 8, 512], F32))
        kb = KB(nc)

        def aview(off, cnt, **dims):
            v = arena[:, off:off + cnt]
            if dims:
                names = list(dims)
                pat = "p (" + " ".join(names) + ") -> p " + " ".join(names)
                v = v.rearrange(pat, **{k: dims[k] for k in names[:-1]})
            return v

        uT = aview(OFF_RA, RA_N, k=8, t=TT)
        u = aview(OFF_RB, RB_N, k=8, t=TT)
        z_t = aview(OFF_RB, 8 * NT, k=8, t=NT)
        s_t = aview(OFF_RD, RD_N, k=8, t=NT)
        sqb = aview(OFF_RC, RC_N, b=2, t=NT * 2)
        Dm = aview(OFF_RDM, 2 * 31 * 128, b=2, j=31, q=128)
        m1 = aview(OFF_RC, RC_N + RDM_N, k=8, t=NT)
        ptm = aview(OFF_RC, 4 * 1024, b=4, n=1024)
        h1 = aview(OFF_RC, 16384, f=32, t=512)
        hT_all = stat[:].rearrange("p a n -> p (a n)").bitcast(BF16).rearrange("p (k t) -> p k t", k=8)

        def slot(i):
            return aview(OFF_RW + i * 2048, 2048, k=8, n=256)

        def wff1_off(c):
            return (OFF_RA + c * 4096) if c < 4 else (OFF_RD + (c - 4) * 4096)

        def wff1_blk(c):
            return aview(wff1_off(c), 4096, k=8, n=512)

        def wff2_off(f):
            return (OFF_RB + f * 1024) if f < 16 else (OFF_RW + (f - 16) * 1024)

        def wff2_fc(f):
            return aview(wff2_off(f), 1024)

        WO_SL = 0
        wo_t = aview(OFF_RW, 8192, h=2, k=8, n=512)

        for k in range(8):
            kb.reg(("uT", k, -1), OFF_RA + k * TT, H)
            for b in range(4):
                kb.reg(("uT", k, b), OFF_RA + k * TT + H + b * 512, 512)
            kb.reg(("u", k), OFF_RB + k * TT, TT)
            for b in range(4):
                kb.reg(("z", k, b), OFF_RB + k * NT + b * 512, 512)
            kb.reg(("s", k), OFF_RD + k * NT, NT)
            kb.reg(("m1", k), OFF_RC + k * NT, NT)
            kb.reg(("rw", k), OFF_RW + k * 2048, 2048)
            kb.reg(("wff1", k), wff1_off(k), 4096)
        for b in range(2):
            kb.reg(("sqb", b), OFF_RC + b * 4096, 4096)
            kb.reg(("Dm", b), OFF_RDM + b * 3968, 3968)
            kb.reg(("wo", b), OFF_RW + b * 4096, 4096)
        for b in range(4):
            kb.reg(("ptm", b), OFF_RC + b * 1024, 1024)
        for f in range(32):
            kb.reg(("wff2", f), wff2_off(f), 1024)
            kb.reg(("h1", f), OFF_RC + f * 512, 512)

        st = {"bank": 0, "k": 0, "alt": 0, "xi": 0, "ti": 0}

        def nb():
            b = st["bank"]
            st["bank"] = (b + 1) % 8
            return b

        def PSK(b):
            return ("ps", b)

        def alt():
            st["alt"] ^= 1
            return "act" if st["alt"] else "dve"

        def nti():
            st["ti"] = (st["ti"] + 1) % 4
            return st["ti"]

        def MM(out_ap, lhsT, rhs, start, stop, reads, writes):
            kb.op("pe", lambda e: e.matmul(out_ap, lhsT=lhsT, rhs=rhs, start=start, stop=stop), reads=reads, writes=writes)

        def TR(out_ap, in_ap, ident_ap, reads, writes):
            kb.op("pe", lambda e: e.transpose(out_ap, in_ap, ident_ap), reads=reads, writes=writes)

        def ACT(out_ap, in_ap, func, reads, writes, bias=None, scale=None, accum=None):
            kw = {}
            if bias is not None:
                kw["bias"] = bias
            if scale is not None:
                kw["scale"] = scale
            if accum is not None:
                kw["accum_out"] = accum
            kb.op("act", lambda e: e.activation(out=out_ap, in_=in_ap, func=func, **kw), reads=reads, writes=writes)

        def TTOP(eng, out_ap, in0, in1, op, reads, writes):
            kb.op(eng, lambda e: e.tensor_tensor(out=out_ap, in0=in0, in1=in1, op=op), reads=reads, writes=writes)

        def TS(eng, out_ap, in0, s1, op0, reads, writes):
            kb.op(eng, lambda e: e.tensor_scalar(out=out_ap, in0=in0, scalar1=s1, scalar2=None, op0=op0), reads=reads, writes=writes)

        def STT(out_ap, in0, scalar, in1, op0, op1, reads, writes):
            kb.op("dve", lambda e: e.scalar_tensor_tensor(out=out_ap, in0=in0, scalar=scalar, in1=in1, op0=op0, op1=op1),
                  reads=reads, writes=writes)

        def RECIP(out_ap, in_ap, reads, writes):
            kb.op("dve", lambda e: e.reciprocal(out_ap, in_ap), reads=reads, writes=writes)

        def COPY(eng, out_ap, in_ap, reads, writes):
            if eng == "act":
                ACT(out_ap, in_ap, AF.Copy, reads, writes)
            else:
                kb.op(eng, lambda e: e.tensor_copy(out_ap, in_ap), reads=reads, writes=writes)

        def DMA(eng, out_ap, in_ap, reads, writes, after=()):
            kb.op(eng, lambda e: e.dma_start(out=out_ap, in_=in_ap), reads=reads, writes=writes, dma=True, after=after)

        def wload(sl_ap, src_ap, keys, after=()):
            DMA("pool", sl_ap, src_ap, [], keys, after=after)

        w_in_v = w_in.rearrange("(k p) n -> p k n", p=128)
        wco_v = w_conv_out.rearrange("(k p) n -> p k n", p=128)
        wpo_v = w_pool_out.rearrange("(k p) n -> p k n", p=128)
        wo_v = w_o.rearrange("(k p) n -> p k n", p=128)
        wff1_v = w_ff1.rearrange("(k p) n -> p k n", p=128)
        wff2_v = w_ff2.rearrange("(f p) n -> p f n", p=128)

        DMA("sp", gb[:, 0, :], gbd[0:1, :].partition_broadcast(128), [], [("gb", 0)])
        DMA("sp", identf[:], identd, [], ["identf"])
        xs = aview(OFF_RD, 5 * 2048).bitcast(F32).rearrange("p (b n) -> p b n", b=5)
        for b in range(5):
            kb.reg(("xs", b), OFF_RD + b * 2048, 2048)
        xs_ev = {}

        def s0_stage(j):
            P = 32 if j < 0 else 128
            r0 = 0 if j < 0 else H + 128 * j
            b = 4 if j < 0 else j
            xs_ev[j] = kb.op("sp", lambda e: e.dma_start(out=xs[:P, b, :], in_=xh[r0:r0 + P, :]), writes=[("xs", b)], dma=True)

        for j in (0, 1, 2, 3, -1):
            s0_stage(j)
        DMA("sp", vec[:], vecs, [], ["vec"])
        COPY("dve", identb[:], identf[:], ["identf"], ["identb"])
        kb.op("dve", lambda e: e.memset(onesb[:], 1.0), writes=["onesb"])
        kb.op("dve", lambda e: e.memset(epst[:, 0:1], RMS_EPS), writes=["eps0"])
        kb.op("dve", lambda e: e.memset(epst[:, 1:2], LN_EPS), writes=["eps1"])
        kb.op("dve", lambda e: e.memset(mhalf[:], -0.5), writes=["mhalf"])
        kT_v = vec[:, 32:280].rearrange("p (c j) -> p c j", c=8)

        def vcol(base, cc):
            return vec[:, base + cc:base + cc + 1]

        def rstd_col(P, k, epsc):
            ACT(sm[:P, 16 + k:17 + k], sm[:P, k:k + 1], AF.Sqrt, [("sm", k), "eps%d" % epsc], [("sm", 16 + k)],
                bias=epst[:P, epsc:epsc + 1], scale=1.0 / DM)
            RECIP(sm[:P, 32 + k:33 + k], sm[:P, 16 + k:17 + k], [("sm", 16 + k)], [("sm", 32 + k)])
            return sm[:P, 32 + k:33 + k], ("sm", 32 + k)

        def rstd_col_pool(P, k):
            kb.op("pool", lambda e: e.tensor_scalar(out=sm[:P, 16 + k:17 + k], in0=sm[:P, k:k + 1], scalar1=1.0 / DM,
                                                     scalar2=RMS_EPS, op0=ALU.mult, op1=ALU.add),
                  reads=[("sm", k)], writes=[("sm", 16 + k)])
            TTOP("pool", sm[:P, 32 + k:33 + k], sm[:P, 16 + k:17 + k], mhalf[:P, 0:1], ALU.pow,
                 [("sm", 16 + k), "mhalf"], [("sm", 32 + k)])
            return sm[:P, 32 + k:33 + k], ("sm", 32 + k)

        def next_k():
            k = st["k"]
            st["k"] = (k + 1) % 16
            return k

        junk = junkb
        JUNKK = ["junkb"]

        def norm_part(P, src_ap, src_key, gslot, pool_pow=False):
            k = next_k()
            bi = st["xi"]
            st["xi"] = (bi + 1) % 3
            ACT(junk[:P, :], src_ap, AF.Square, [src_key], JUNKK + [("sm", k)], accum=sm[:P, k:k + 1])
            rs, rsk = rstd_col_pool(P, k) if pool_pow else rstd_col(P, k, 0)
            STT(xb[:P, bi, :], src_ap, rs, gb[:P, gslot, :], ALU.mult, ALU.mult, [src_key, rsk, ("gb", gslot)], [("xb", bi)])
            return bi

        def transp_part(P, bi, dst_ap, dst_keys):
            b = nb()
            psb = ps[:, b, :].bitcast(BF16)
            for kc in range(8):
                TR(psb[:, kc * 128:kc * 128 + P], xb[:P, bi, kc * 128:(kc + 1) * 128], identb[:P, :P],
                   [("xb", bi), "identb"], [PSK(b)])
            src = psb.rearrange("p (k t) -> p k t", k=8)[:, :, 0:P]
            COPY(alt(), dst_ap, src, [PSK(b)], dst_keys)

        def s0_norm(j):
            P = 32 if j < 0 else 128
            r0 = 0 if j < 0 else H + 128 * j
            if j < 4:
                b = 4 if j < 0 else j
                return norm_part(P, xs[:P, b, :], ("xs", b), 0)
            xi = (j + 1) % 2
            DMA("sp", xt[:P, xi, :], xh[r0:r0 + P, :], [], [("xt", xi)])
            return norm_part(P, xt[:P, xi, :], ("xt", xi), 0, pool_pow=True)

        def s0_transp(j, bi):
            P = 32 if j < 0 else 128
            r0 = 0 if j < 0 else H + 128 * j
            blk = -1 if j < 0 else j // 4
            transp_part(P, bi, uT[:, :, r0:r0 + P], [("uT", k, blk) for k in range(8)])

        def glu_group(cc, t0, n, blk):
            cp, ccl = cc // 2, cc % 2
            sa, sg, ka, kg = slot(cp), slot(4 + cp), ("rw", cp), ("rw", 4 + cp)
            if n == H:
                bh = nb()
                pa, pg, ra, rg = ps[:, bh, 0:H], ps[:, bh, H:2 * H], PSK(bh), PSK(bh)
            else:
                ba, bg = nb(), nb()
                pa, pg, ra, rg = ps[:, ba, :], ps[:, bg, :], PSK(ba), PSK(bg)
            for (po, sl, key, rk) in ((pa, sa, ka, ra), (pg, sg, kg, rg)):
                for kc in range(8):
                    MM(po, sl[:, kc, ccl * 128:(ccl + 1) * 128], uT[:, kc, t0:t0 + n], kc == 0, kc == 7,
                       [("uT", kc, blk), key], [rk])
            t_i = nti()
            ACT(tmpA[:, t_i, 0:n], pg, AF.Sigmoid, [rg], [("tmpA", t_i)])
            TTOP("dve", u[:, cc, t0:t0 + n], pa, tmpA[:, t_i, 0:n], ALU.mult, [ra, ("tmpA", t_i)], [("u", cc)])

        for cp in range(4):
            aft = [xs_ev[1]] if cp == 0 else [xs_ev[-1]]
            wload(slot(cp), w_in_v[:, :, cp * 256:(cp + 1) * 256], [("rw", cp)], after=aft)
            wload(slot(4 + cp), w_in_v[:, :, 1024 + cp * 256:1024 + (cp + 1) * 256], [("rw", 4 + cp)], after=aft)
        DMA("pool", pmb[:].rearrange("p g n -> p (g n)"), pmd, [], ["pmb"])
        pend = None
        for j in (0, 1, 2, 3, -1):
            bi_c = s0_norm(j)
            if pend is not None:
                s0_transp(*pend)
            pend = (j, bi_c)
        s0_transp(*pend)
        for blk in range(4):
            for cc in range(8):
                nxt = 4 * (blk + 1) + cc // 2
                if cc % 2 == 0 and nxt < 16:
                    bi_n = s0_norm(nxt)
                glu_group(cc, H + blk * 512, 512, blk)
                if cc % 2 == 1 and nxt < 16:
                    s0_transp(nxt, bi_n)
            if blk == 0:
                for cc in range(8):
                    glu_group(cc, 0, H, -1)

        def gate_load(cp, base, wmat_v, gcol0):
            wload(slot(base), wmat_v[:, :, cp * 256:(cp + 1) * 256], [("rw", base)])
            wload(slot(base + 1), w_in_v[:, :, gcol0 + cp * 256:gcol0 + (cp + 1) * 256], [("rw", base + 1)])

        CB = OFF_RC
        utm = [aview(CB + b * 2048, 2048, g=16, r=16, c=8) for b in range(2)]
        uhtm = aview(CB + 4096, 2048, g=16, r=16, c=8)
        UTt = [aview(CB + 6144 + b * 2080, 2080, g=16, m=130) for b in range(2)]
        cvtm = [aview(CB + 10304 + b * 2048, 2048, r=16, c=128) for b in range(2)]
        Wc = [aview(OFF_RW + b * 6144, 6144, g=16, d=3, n=128) for b in range(2)]
        for b in range(2):
            kb.reg(("utm", b), CB + b * 2048, 2048)
            kb.reg(("UT", b), CB + 6144 + b * 2080, 2080)
            kb.reg(("cvtm", b), CB + 10304 + b * 2048, 2048)
            kb.reg(("Wc", b), OFF_RW + b * 6144, 6144)
        kb.reg("uhtm", CB + 4096, 2048)

        def pbank():
            b = nb()
            return b, ps[:, b, :].bitcast(BF16)

        cva = {"i": 0}

        def calt():
            cva["i"] = (cva["i"] + 1) % 3
            return "act" if cva["i"] == 0 else "dve"

        def cv_load(cc):
            DMA("pool", Wc[cc % 2].rearrange("p g d n -> p (g d n)"), convw[cc], [], [("Wc", cc % 2)])

        def cv_t1(cc):
            bf = cc % 2
            um = u[:, cc, H:H + NT].rearrange("p (m r) -> p m r", r=16)
            uh = u[:, cc, 0:H].rearrange("p (m r) -> p m r", r=16)
            for half in range(2):
                b, pb = pbank()
                for r8 in range(8):
                    r = half * 8 + r8
                    TR(pb[:, r8 * 128:(r8 + 1) * 128], um[:, :, r], identb[:], [("u", cc), "identb"], [PSK(b)])
                COPY(calt(), utm[bf][:, :, half * 8:half * 8 + 8, :].rearrange("p g r c -> p r g c"),
                     pb.rearrange("p (r g c) -> p r g c", r=8, g=16), [PSK(b)], [("utm", bf)])
            for half in range(2):
                b, pb = pbank()
                for r8 in range(8):
                    r = half * 8 + r8
                    TR(pb[:2, r8 * 128:(r8 + 1) * 128], uh[:, :, r], identb[:], [("u", cc), "identb"], [PSK(b)])
                COPY(calt(), uhtm[:2, :, half * 8:half * 8 + 8, :].rearrange("p g r c -> p r g c"),
                     pb[:2, :].rearrange("p (r g c) -> p r g c", r=8, g=16), [PSK(b)], ["uhtm"])

        def cv_t2(cc):
            bf = cc % 2
            for half in range(2):
                b, pb = pbank()
                for g8 in range(8):
                    gl = half * 8 + g8
                    TR(pb[:, g8 * 128:(g8 + 1) * 128], utm[bf][:, gl, :, :].rearrange("p r c -> p (r c)"), identb[:],
                       [("utm", bf), "identb"], [PSK(b)])
                COPY(calt(), UTt[bf][:, half * 8:half * 8 + 8, 2:130], pb.rearrange("p (g m) -> p g m", g=8), [PSK(b)], [("UT", bf)])
            b, pb = pbank()
            for gl in range(16):
                TR(pb[:, gl * 2:gl * 2 + 2], uhtm[:2, gl, :, :].rearrange("p r c -> p (r c)"), identb[:2, :2], ["uhtm", "identb"], [PSK(b)])
            COPY(calt(), UTt[bf][:, :, 0:2], pb[:, 0:32].rearrange("p (g m) -> p g m", g=16), [PSK(b)], [("UT", bf)])

        def cv_mm(cc):
            bf = cc % 2
            for q in range(4):
                b = nb()
                for g4 in range(4):
                    gl = q * 4 + g4
                    for d in range(3):
                        MM(ps[:, b, g4 * 128:(g4 + 1) * 128], UTt[bf][:, gl, d:d + 128], Wc[bf][:, gl, d, :], d == 0, d == 2,
                           [("UT", bf), ("Wc", bf)], [PSK(b)])
                src = ps[:, b, :].rearrange("p (g r c) -> p g r c", g=4, r=16)
                dst = cvtm[bf][:, :, q * 32:(q + 1) * 32].rearrange("p r (g c) -> p g r c", g=4)
                COPY(calt(), dst, src, [PSK(b)], [("cvtm", bf)])

        def cv_t3(cc):
            bf = cc % 2
            sv = s_t[:, cc, :].rearrange("p (m r) -> p m r", r=16)
            for half in range(2):
                b, pb = pbank()
                for r8 in range(8):
                    r = half * 8 + r8
                    TR(pb[:, r8 * 128:(r8 + 1) * 128], cvtm[bf][:, r, :], identb[:], [("cvtm", bf), "identb"], [PSK(b)])
                if half == 0:
                    ACT(sv[:, :, 0:8], pb.rearrange("p (r m) -> p m r", r=8), AF.Identity, [PSK(b), "vec"], [("s", cc)],
                        bias=vcol(0, cc))
                else:
                    TS("dve", sv[:, :, 8:16], pb.rearrange("p (r m) -> p m r", r=8), vcol(0, cc), ALU.add, [PSK(b), "vec"], [("s", cc)])

        cv_load(0)
        cv_load(1)
        for step in range(8 + 3):
            if 0 <= step - 1 < 8:
                cv_t2(step - 1)
            if step < 8:
                cv_t1(step)
            if 0 <= step - 2 < 8:
                cv_mm(step - 2)
                if step - 2 + 2 < 8:
                    cv_load(step - 2 + 2)
            if 0 <= step - 3 < 8:
                cv_t3(step - 3)

        if dbg == "cv":
            DMA("sp", dbg_cv, s_t, [("s", k) for k in range(8)], ["dbg_cv"])
            kb.final_wait("sp", ["dbg_cv"])
            sems = {}
            for e in ("pe", "act", "dve", "pool"):
                sems[e] = es.enter_context(nc.semaphore("s_" + e))
            for i in range(KB.N_DMA_SEM):
                sems[("d", i)] = es.enter_context(nc.semaphore("s_d%d" % i))
            for i in range(kb.n_sw):
                sems[("w", i)] = es.enter_context(nc.semaphore("s_w%d" % i))
            block = es.enter_context(nc.Block())
            kb.emit(sems, block)
            return nc

        for q in range(4):
            wload(slot(4 + q), w_in_v[:, :, 2048 + q * 256:2048 + (q + 1) * 256], [("rw", 4 + q)])
        gate_load(0, 0, wpo_v, 4096)
        gate_load(1, 2, wpo_v, 4096)

        for cc in range(8):
            sb = cc % 2
            if cc % 4 == 1:
                ACT(sqb[:, sb, 0:NT], s_t[:, cc, :], AF.Square, [("s", cc)], [("sqb", sb)])
            else:
                TTOP("dve", sqb[:, sb, 0:NT], s_t[:, cc, :], s_t[:, cc, :], ALU.mult, [("s", cc)], [("sqb", sb)])
            for blk in range(4):
                MM(ps[:, blk, :], onesb[:], s_t[:, cc, blk * 512:(blk + 1) * 512], cc == 0, cc == 7,
                   ["onesb", ("s", cc)], [PSK(blk)])
                MM(ps[:, 4 + blk, :], onesb[:], sqb[:, sb, blk * 512:(blk + 1) * 512], cc == 0, cc == 7,
                   ["onesb", ("sqb", sb)], [PSK(4 + blk)])
        st["bank"] = 0
        gbv = gb[:].rearrange("p a (b n) -> p (a b) n", b=2)

        def ln_stats(blk):
            mean, var = stat[:, blk, :], gbv[:, blk, :]
            t_i = nti()
            TS("dve", mean, ps[:, blk, :], 1.0 / DM, ALU.mult, [PSK(blk)], [("stat", blk)])
            TTOP("dve", tmpA[:, t_i, :], mean, mean, ALU.mult, [("stat", blk)], [("tmpA", t_i)])
            STT(var, ps[:, 4 + blk, :], 1.0 / DM, tmpA[:, t_i, :], ALU.mult, ALU.subtract, [PSK(4 + blk), ("tmpA", t_i)], [("gb", blk // 2)])

        def ln_stats_finish(blk):
            var = gbv[:, blk, :]
            ACT(var, var, AF.Sqrt, [("gb", blk // 2), "eps1"], [("gb", blk // 2)], bias=epst[:, 1:2], scale=1.0)
            RECIP(var, var, [("gb", blk // 2)], [("gb", blk // 2)])

        def ln_unit(blk, cc):
            mean, var = stat[:, blk, :], gbv[:, blk, :]
            t_i = nti()
            sl = s_t[:, cc, blk * 512:(blk + 1) * 512]
            TTOP("dve", tmpA[:, t_i, :], sl, mean, ALU.subtract, [("s", cc), ("stat", blk)], [("tmpA", t_i)])
            TTOP("dve", tmpA[:, t_i, :], tmpA[:, t_i, :], var, ALU.mult, [("tmpA", t_i), ("gb", blk // 2)], [("tmpA", t_i)])
            ACT(sl, tmpA[:, t_i, :], AF.Silu, [("tmpA", t_i), "vec"], [("s", cc)], bias=vcol(16, cc), scale=vcol(8, cc))

        for blk in range(4):
            ln_stats(blk)
        ln_stats_finish(0)

        def s4a_proj(j):
            P = 32 if j < 0 else 128
            t0 = 0 if j < 0 else H + 128 * j
            blk = -1 if j < 0 else j // 4
            pb = 3 if j < 0 else j % 3
            bq = (nb(), nb())
            for q in range(4):
                po = ps[:P, bq[q // 2], (q % 2) * 256:(q % 2) * 256 + 256]
                for kc in range(8):
                    MM(po, uT[:, kc, t0:t0 + P], slot(4 + q)[:, kc, :], kc == 0, kc == 7, [("uT", kc, blk), ("rw", 4 + q)],
                       [PSK(bq[q // 2])])
            for hh in range(2):
                COPY(alt(), ptm[:P, pb, hh * 512:(hh + 1) * 512], ps[:P, bq[hh], :], [PSK(bq[hh])], [("ptm", pb)])

        def s4a_pool(j):
            pb = j % 3
            bz = (nb(), nb())
            ppb = 3 if j == 0 else (j - 1) % 3
            PP = 32 if j == 0 else 128
            for cc in range(8):
                g = cc // 2
                zo = ps[:, bz[cc // 4], (cc % 4) * 128:(cc % 4) * 128 + 128]
                acol = 0 if j == 0 else 128
                MM(zo, ptm[:, pb, cc * 128:(cc + 1) * 128], pmb[:, g, acol:acol + 128], True, False,
                   [("ptm", pb), "pmb"], [PSK(bz[cc // 4])])
                pcol = 272 if j == 0 else 256
                MM(zo[:, 0:16], ptm[:PP, ppb, cc * 128:(cc + 1) * 128], pmb[:PP, g, pcol:pcol + 16], False, True,
                   [("ptm", ppb), "pmb"], [PSK(bz[cc // 4])])
            for hh in range(2):
                COPY(alt(), z_t[:, 4 * hh:4 * hh + 4, j * 128:(j + 1) * 128],
                     ps[:, bz[hh], :].rearrange("p (c t) -> p c t", c=4), [PSK(bz[hh])], [("z", 4 * hh + c, j // 4) for c in range(4)])

        s4a_proj(-1)
        s4a_proj(0)
        for blk in range(4):
            if blk + 1 < 4:
                ln_stats_finish(blk + 1)
            for tl in range(4):
                j = 4 * blk + tl
                ln_unit(blk, 2 * tl)
                if j + 1 < 16:
                    s4a_proj(j + 1)
                ln_unit(blk, 2 * tl + 1)
                s4a_pool(j)

        pw_v = slot(4).rearrange("p k n -> p (k n)").rearrange("p (g k n) -> p g k n", g=4, k=2)
        wload(pw_v, pool_w.rearrange("g (k p) n -> p g k n", p=128), [("rw", 4)])
        gate_load(0, 6, wco_v, 3072)
        for g in range(4):
            for blk in range(4):
                bo = (nb(), nb())
                for oc in range(2):
                    for k2 in range(2):
                        MM(ps[:, bo[oc], :], pw_v[:, g, k2, oc * 128:(oc + 1) * 128], z_t[:, 2 * g + k2, blk * 512:(blk + 1) * 512],
                           k2 == 0, k2 == 1, [("rw", 4), ("z", 2 * g + k2, blk)], [PSK(bo[oc])])
                ACT(z_t[:, 2 * g, blk * 512:(blk + 1) * 512], ps[:, bo[0], :], AF.Identity, [PSK(bo[0]), "vec"],
                    [("z", 2 * g, blk)], scale=vcol(24, 2 * g))
                TS("dve", z_t[:, 2 * g + 1, blk * 512:(blk + 1) * 512], ps[:, bo[1], :], vcol(24, 2 * g + 1), ALU.mult,
                   [PSK(bo[1]), "vec"], [("z", 2 * g + 1, blk)])

        def gated_group(cc, ccl, base, src, srck, blk, first):
            sw, sgt, kw, kg = slot(base), slot(base + 1), ("rw", base), ("rw", base + 1)
            by, bg = nb(), nb()
            for kc in range(8):
                MM(ps[:, by, :], sw[:, kc, ccl * 128:(ccl + 1) * 128], src[:, kc, blk * 512:(blk + 1) * 512], kc == 0, kc == 7,
                   [(srck, kc, blk) if srck == "z" else (srck, kc), kw], [PSK(by)])
            for kc in range(8):
                MM(ps[:, bg, :], sgt[:, kc, ccl * 128:(ccl + 1) * 128], uT[:, kc, H + blk * 512:H + (blk + 1) * 512], kc == 0, kc == 7,
                   [("uT", kc, blk), kg], [PSK(bg)])
            t_i = nti()
            ACT(tmpA[:, t_i, :], ps[:, bg, :], AF.Sigmoid, [PSK(bg)], [("tmpA", t_i)])
            dst = m1[:, cc, blk * 512:(blk + 1) * 512]
            if first:
                TTOP("dve", dst, ps[:, by, :], tmpA[:, t_i, :], ALU.mult, [PSK(by), ("tmpA", t_i)], [("m1", cc)])
            else:
                TTOP("dve", tmpA[:, t_i, :], ps[:, by, :], tmpA[:, t_i, :], ALU.mult, [PSK(by), ("tmpA", t_i)], [("tmpA", t_i)])
                TTOP("pool" if (blk + cc) % 2 else "dve", dst, tmpA[:, t_i, :], dst, ALU.add, [("tmpA", t_i), ("m1", cc)], [("m1", cc)])

        for cp in range(4):
            base = (cp % 2) * 2
            for ccl in range(2):
                for blk in range(4):
                    gated_group(2 * cp + ccl, ccl, base, z_t, "z", blk, True)
            if cp + 2 < 4:
                gate_load(cp + 2, base, wpo_v, 4096)
            if cp == 0:
                gate_load(1, 4, wco_v, 3072)
        for c in range(4):
            wload(aview(OFF_RB + c * 4096, 4096, f=4, n=1024), wff2_v[:, 4 * c:4 * c + 4, :], [("wff2", f) for f in range(4 * c, 4 * c + 4)])
        for cp in range(4):
            base = 6 - (cp % 2) * 2
            for ccl in range(2):
                for blk in range(4):
                    gated_group(2 * cp + ccl, ccl, base, s_t, "s", blk, False)
            if cp + 2 < 4:
                gate_load(cp + 2, base, wco_v, 3072)
            if cp == 1:
                for hh in range(2):
                    wload(wo_t[:, hh, :, :], wo_v[:, :, hh * 512:(hh + 1) * 512], [("wo", hh)])

        def ff1_load(c):
            wload(wff1_blk(c), wff1_v[:, :, c * 512:(c + 1) * 512], [("wff1", c)])

        def nb2():
            return (nb(), nb())

        def resid_a(b2, gslot, tb):
            k = next_k()
            k2 = next_k()
            tt = tmpA[:, 2 * tb:2 * tb + 2, :].rearrange("p a n -> p (a n)")
            kk = [("tmpA", 2 * tb), ("tmpA", 2 * tb + 1)]
            def tt_half(hh):
                TTOP("dve", tt[:, hh * 512:(hh + 1) * 512], ps[:, b2[hh], :], gb[:, gslot, hh * 512:(hh + 1) * 512], ALU.mult,
                     [PSK(b2[hh]), ("gb", gslot)], [("tmpA", 2 * tb + hh)])

            tt_half(0)
            ACT(junk[:, 512:1024], ps[:, b2[1], :], AF.Square, [PSK(b2[1])], JUNKK + [("sm", k2)], accum=sm[:, k2:k2 + 1])
            tt_half(1)
            ACT(junk[:, 0:512], ps[:, b2[0], :], AF.Square, [PSK(b2[0])], JUNKK + [("sm", k)], accum=sm[:, k:k + 1])
            TTOP("dve", sm[:, k:k + 1], sm[:, k:k + 1], sm[:, k2:k2 + 1], ALU.add, [("sm", k), ("sm", k2)], [("sm", k)])
            ACT(sm[:, 16 + k:17 + k], sm[:, k:k + 1], AF.Sqrt, [("sm", k), "eps0"], [("sm", 16 + k)],
                bias=epst[:, 0:1], scale=1.0 / DM)
            return (k, tt, kk)

        def resid_b(ctx, res_ap, res_key, dst_rows, outk):
            k, tt, kk = ctx
            RECIP(sm[:, 32 + k:33 + k], sm[:, 16 + k:17 + k], [("sm", 16 + k)], [("sm", 32 + k)])
            STT(res_ap, tt, sm[:, 32 + k:33 + k], res_ap, ALU.mult, ALU.add, kk + [("sm", 32 + k), res_key], [res_key])
            DMA("sp", out[dst_rows:dst_rows + 128, :], res_ap, [res_key], [outk])

        DMA("sp", gb[:, 1, :], gbd[1:2, :].partition_broadcast(128), [], [("gb", 1)])
        DMA("sp", gb[:, 0, :], gbd[2:3, :].partition_broadcast(128), [], [("gb", 0)])
        for c in range(7):
            ff1_load(c)
        xr = aview(OFF_RD + 12288, 4096).bitcast(F32).rearrange("p (b n) -> p b n", b=2)
        kb.reg(("xt", 2), OFF_RD + 12288, 2048)
        kb.reg(("xt", 3), OFF_RD + 14336, 2048)

        def xbuf(i):
            return xt[:, i, :] if i < 2 else xr[:, i - 2, :]

        def s6_transp(j, bi):
            tl = j % 4
            transp_part(128, bi, hT_all[:, :, tl * 128:(tl + 1) * 128], [("hT", tl)])

        def s5_a(j):
            bi = j % 4
            DMA("sp", xbuf(bi), xh[H + 128 * j:H + 128 * (j + 1), :], [], [("xt", bi)])
            b2 = nb2()
            for hh in range(2):
                for kc in range(8):
                    MM(ps[:, b2[hh], :], m1[:, kc, j * 128:(j + 1) * 128], wo_t[:, hh, kc, :], kc == 0, kc == 7,
                       [("m1", kc), ("wo", hh)], [PSK(b2[hh])])
            return resid_a(b2, 1, j % 2)

        def s5_b(j, ctx):
            bi = j % 4
            resid_b(ctx, xbuf(bi), ("xt", bi), 128 * j, ("outh", j))
            if j < 4:
                s5_pend.append((j, norm_part(128, xbuf(bi), ("xt", bi), 0)))
            if s5_pend and s5_pend[0][0] <= j - 2:
                s6_transp(*s5_pend.pop(0))

        s5_pend = []
        pend = None
        for j in range(16):
            ctx = s5_a(j)
            if pend is not None:
                s5_b(*pend)
            pend = (j, ctx)
        s5_b(*pend)
        ff1_load(7)
        for c in range(4, 8):
            wload(aview(OFF_RW + (c - 4) * 4096, 4096, f=4, n=1024), wff2_v[:, 4 * c:4 * c + 4, :],
                  [("wff2", f) for f in range(4 * c, 4 * c + 4)])

        DMA("sp", gb[:, 1, :], gbd[3:4, :].partition_broadcast(128), [], [("gb", 1)])

        def s6_norm(j):
            DMA("sp", xt[:, 0, :], out[128 * j:128 * (j + 1), :], [("outh", j)], [("xt", 0)])
            return norm_part(128, xt[:, 0, :], ("xt", 0), 0)

        for blk in range(4):
            for fc in range(32):
                b = nb()
                c, fl = fc // 4, fc % 4
                wv = wff1_blk(c)
                for kc in range(8):
                    MM(ps[:, b, :], wv[:, kc, fl * 128:(fl + 1) * 128], hT_all[:, kc, :], kc == 0, kc == 7,
                       [("wff1", c)] + [("hT", t) for t in range(4)], [PSK(b)])
                t_i = nti() % 2
                ACT(tmpA[:, t_i, :], ps[:, b, :], AF.Relu, [PSK(b)], [("tmpA", t_i)])
                TTOP("dve" if fc % 2 else "pool", h1[:, fc, :], tmpA[:, t_i, :], tmpA[:, t_i, :], ALU.mult, [("tmpA", t_i)], [("h1", fc)])
            pend = None
            for tl in range(4):
                j = 4 * blk + tl
                nj = j + 4
                if nj < 16:
                    bi_n = s6_norm(nj)
                DMA("sp", xt[:, 1, :], out[128 * j:128 * (j + 1), :], [("outh", j)], [("xt", 1)])
                b2 = nb2()
                for hh in range(2):
                    for fc in range(32):
                        MM(ps[:, b2[hh], :], h1[:, fc, tl * 128:(tl + 1) * 128], wff2_fc(fc)[:, hh * 512:(hh + 1) * 512],
                           fc == 0, fc == 31, [("h1", fc), ("wff2", fc)], [PSK(b2[hh])])
                if nj < 16:
                    s6_transp(nj, bi_n)
                ctx = resid_a(b2, 1, 1)
                resid_b(ctx, xt[:, 1, :], ("xt", 1), 128 * j, ("outh", j))

        kb.final_wait("sp", [("outh", j) for j in range(16)])
        sems = {}
        for e in ("pe", "act", "dve", "pool"):
            sems[e] = es.enter_context(nc.semaphore("s_" + e))
        for i in range(KB.N_DMA_SEM):
            sems[("d", i)] = es.enter_context(nc.semaphore("s_d%d" % i))
        for i in range(kb.n_sw):
            sems[("w", i)] = es.enter_context(nc.semaphore("s_w%d" % i))
        block = es.enter_context(nc.Block())
        kb.emit(sems, block)
    return nc


def _pool_mats(first):
    pm = np.zeros((128, 4, 288), np.float32)
    t = np.arange(128)
    for g, w in enumerate(POOL_W):
        d = t[None, :] - t[:, None]
        band = ((d >= 0) & (d < w)).astype(np.float32)
        std = band / w - np.eye(128, dtype=np.float32)
        cnt = np.minimum(t + 1, w).astype(np.float32)
        fst = band / cnt[None, :] - np.eye(128, dtype=np.float32)
        pm[:, g, 0:128] = fst if first else std
        pm[:, g, 128:256] = std
        tp = np.arange(128)[:, None]
        tt = np.arange(16)[None, :]
        prev = ((tt + 128 - tp) < w).astype(np.float32) / w
        pm[:, g, 256:272] = prev
        pm[0:32, g, 272:288] = prev[96:128, :]
    return pm.reshape(128, 4 * 288)


def _conv_mats(dw_kernel):
    k = np.asarray(dw_kernel, np.float32)
    W = np.zeros((8, 16, 8, 16, 3, 16, 8), np.float32)
    kk = k.reshape(31, 8, 16, 8)
    for d in range(3):
        for r in range(16):
            for rp in range(16):
                j = 16 * (d - 2) + r - rp + 30
                if 0 <= j <= 30:
                    for c in range(8):
                        W[:, r, c, :, d, rp, c] = kk[j, :, :, c]
    return W.reshape(8, 128, 16 * 3 * 128)


def _prep_inputs(inp):
    f = lambda a: np.ascontiguousarray(np.asarray(a, dtype=np.float32))
    x = f(inp["x"])
    cm = lambda v: f(v).reshape(8, 128).T
    vecs = np.zeros((128, 280), np.float32)
    vecs[:, 0:8] = cm(inp["dw_bias"])
    vecs[:, 8:16] = cm(inp["conv_ln_g"])
    vecs[:, 16:24] = cm(inp["conv_ln_b"])
    vecs[:, 24:32] = cm(inp["pool_scale"])
    kT = f(inp["dw_kernel"]).T.reshape(8, 128, 31).transpose(1, 0, 2)
    vecs[:, 32:280] = kT.reshape(128, 248)
    gbm = np.stack([f(inp["mix_pre_g"]), f(inp["mix_post_g"]), f(inp["mlp_pre_g"]), f(inp["mlp_post_g"])])
    shared = {k: f(inp[k]) for k in ("w_in", "w_conv_out", "pool_w", "w_pool_out", "w_o", "w_ff1", "w_ff2")}
    shared["vecs"] = vecs
    shared["gb"] = gbm
    shared["ident"] = np.eye(128, dtype=np.float32)
    shared["convw"] = _conv_mats(inp["dw_kernel"])
    maps = []
    for c in range(NCORES):
        b, half = divmod(c, 2)
        xh = np.zeros((TT, DM), np.float32)
        if half == 1:
            xh[:, :] = x[b, NT - H:2 * NT, :]
        else:
            xh[H:, :] = x[b, 0:NT, :]
        m = dict(shared)
        m["xh"] = xh
        m["pm"] = _pool_mats(first=(half == 0))
        maps.append(m)
    return maps


def kernel(**inputs):
    maps = _prep_inputs(inputs)
    nc = build_program()
    res = run_bass_kernel_spmd(nc, maps, core_ids=list(range(NCORES)))
    outs = [np.asarray(r["out"], dtype=np.float32) for r in res.results]
    y = np.stack(outs).reshape(4, 2 * NT, DM)
    return y
```

```python
import numpy as np
from contextlib import ExitStack
import concourse.bass as bass
import concourse.mybir as mybir
from concourse.bass_utils import run_bass_kernel_spmd

F32 = mybir.dt.float32
BF16 = mybir.dt.bfloat16
AF = mybir.ActivationFunctionType
ALU = mybir.AluOpType
AX = mybir.AxisListType

H = 32
NT = 2048
TT = H + NT
DM = 1024
NCORES = 8
RMS_EPS = 1e-6
LN_EPS = 1e-5
POOL_W = (2, 4, 8, 16)

OFF_RA, RA_N = 0, 8 * TT
OFF_RB, RB_N = OFF_RA + RA_N, 8 * TT
OFF_RD, RD_N = OFF_RB + RB_N, 8 * NT
OFF_RW, RW_N = OFF_RD + RD_N, 8 * 2048
OFF_RC, RC_N = OFF_RW + RW_N, 8192
OFF_RDM, RDM_N = OFF_RC + RC_N, 8192
ARENA_N = OFF_RDM + RDM_N


class KB:
    ENG = ("pe", "act", "dve", "pool", "sp")
    N_DMA_SEM = 16

    def __init__(self, nc):
        self.nc = nc
        self.q = {e: [] for e in self.ENG}
        self.count = {e: 0 for e in self.ENG}
        self.last_write = {}
        self.readers = {}
        self.waited = {e: {} for e in self.ENG}
        self.dma_val = [0] * self.N_DMA_SEM
        self.dma_next = 0
        self.n_sw = 0
        self.ranges = {}

    def reg(self, name, start, n):
        self.ranges[name] = (start, start + n)

    def _need(self, eng, ev, waits):
        sk, v = ev
        if sk == "pe" and eng == "pe":
            return
        if self.waited[eng].get(sk, 0) >= v:
            return
        self.waited[eng][sk] = v
        waits[sk] = max(waits.get(sk, 0), v)

    def _clobbers(self, w):
        rg = self.ranges.get(w)
        if rg is None:
            return ()
        out = []
        for n2, (s2, e2) in self.ranges.items():
            if n2 != w and s2 < rg[1] and rg[0] < e2 and (n2 in self.last_write or n2 in self.readers):
                out.append(n2)
        return out

    def op(self, eng, fn, reads=(), writes=(), dma=False, after=()):
        waits = {}
        for ev in after:
            self._need(eng, ev, waits)
        writes = list(writes)
        extra = []
        for w in writes:
            extra.extend(self._clobbers(w))
        for r in reads:
            ev = self.last_write.get(r)
            if ev is not None:
                self._need(eng, ev, waits)
            if isinstance(r, tuple) and r[0] == "ps":
                for ev in self.readers.get(r, ()):
                    if ev[0] != eng:
                        self._need(eng, ev, waits)
        for w in writes + extra:
            ev = self.last_write.get(w)
            if ev is not None:
                self._need(eng, ev, waits)
            for ev in self.readers.get(w, ()):
                self._need(eng, ev, waits)
        if dma and eng == "pool":
            ev = (("w", self.n_sw), 16)
            self.n_sw += 1
        elif dma:
            slot = self.dma_next
            self.dma_next = (self.dma_next + 1) % self.N_DMA_SEM
            if self.dma_val[slot] > 0:
                self._need(eng, (("d", slot), self.dma_val[slot]), waits)
            self.dma_val[slot] += 16
            ev = (("d", slot), self.dma_val[slot])
        else:
            self.count[eng] += 1
            ev = (eng, self.count[eng])
        self.q[eng].append((waits, fn, ev, dma))
        for w in writes:
            self.last_write[w] = ev
            self.readers[w] = []
        for w in extra:
            self.last_write.pop(w, None)
            self.readers.pop(w, None)
        for r in reads:
            self.readers.setdefault(r, []).append(ev)
        return ev

    def final_wait(self, eng, resources):
        waits = {}
        for r in resources:
            ev = self.last_write.get(r)
            if ev is not None:
                self._need(eng, ev, waits)
        self.q[eng].append((waits, None, None, False))

    def emit(self, sems, block):
        deco = {"pe": block.tensor, "act": block.scalar, "dve": block.vector,
                "pool": block.gpsimd, "sp": block.sync}
        needed = {e: set() for e in self.ENG}
        for e in self.ENG:
            for waits, fn, ev, dma in self.q[e]:
                for sk, v in waits.items():
                    if sk in needed:
                        needed[sk].add(v)
        rank = {e: {v: i + 1 for i, v in enumerate(sorted(needed[e]))} for e in self.ENG}
        for e in self.ENG:
            items = self.q[e]

            def body(h, items=items):
                for waits, fn, ev, dma in items:
                    for sk, v in waits.items():
                        h.wait_ge(sems[sk], rank[sk][v] if sk in rank else v)
                    if fn is None:
                        continue
                    inst = fn(h)
                    if dma:
                        inst.then_inc(sems[ev[0]], 16)
                    elif ev[1] in rank[ev[0]]:
                        inst.then_inc(sems[ev[0]], 1)

            deco[e](body)


def build_program(dbg=None):
    nc = bass.Bass("TRN2", target_bir_lowering=False)

    def din(name, shape):
        return nc.dram_tensor(name, shape, F32, kind="ExternalInput").ap()

    xh = din("xh", [TT, DM])
    w_in = din("w_in", [DM, 5120])
    w_conv_out = din("w_conv_out", [DM, DM])
    pool_w = din("pool_w", [4, 256, 256])
    w_pool_out = din("w_pool_out", [DM, DM])
    w_o = din("w_o", [DM, DM])
    w_ff1 = din("w_ff1", [DM, 4096])
    w_ff2 = din("w_ff2", [4096, DM])
    vecs = din("vecs", [128, 32 + 248])
    gbd = din("gb", [4, DM])
    identd = din("ident", [128, 128])
    pmd = din("pm", [128, 4 * 288])
    convw = din("convw", [8, 128, 16 * 3 * 128])
    out = nc.dram_tensor("out", [NT, DM], F32, kind="ExternalOutput").ap()
    dbg_cv = nc.dram_tensor("dbg_cv", [128, 8 * NT], BF16, kind="ExternalOutput").ap() if dbg == "cv" else None

    with ExitStack() as es:
        T = lambda name, shape, dt: es.enter_context(nc.sbuf_tensor(name, shape, dt))
        arena = T("arena", [128, ARENA_N], BF16)
        xt = T("xt", [128, 2, DM], F32)
        xb = T("xb", [128, 3, DM], BF16)
        gb = T("gbt", [128, 2, DM], F32)
        tmpA = T("tmpA", [128, 4, 512], F32)
        stat = T("stat", [128, 4, 512], F32)
        sm = T("sm", [128, 64], F32)
        epst = T("epst", [128, 2], F32)
        vec = T("vec", [128, 32 + 248], F32)
        identf = T("identf", [128, 128], F32)
        identb = T("identb", [128, 128], BF16)
        onesb = T("onesb", [128, 128], BF16)
        pmb = T("pmb", [128, 4, 288], BF16)
        junkb = T("junkb", [128, DM], BF16)
        ps = es.enter_context(nc.psum_tensor("ps", [128, 8, 512], F32))
        kb = KB(nc)

        def aview(off, cnt, **dims):
            v = arena[:, off:off + cnt]
            if dims:
                names = list(dims)
                pat = "p (" + " ".join(names) + ") -> p " + " ".join(names)
                v = v.rearrange(pat, **{k: dims[k] for k in names[:-1]})
            return v

        uT = aview(OFF_RA, RA_N, k=8, t=TT)
        u = aview(OFF_RB, RB_N, k=8, t=TT)
        z_t = aview(OFF_RB, 8 * NT, k=8, t=NT)
        s_t = aview(OFF_RD, RD_N, k=8, t=NT)
        sqb = aview(OFF_RC, RC_N, b=2, t=NT * 2)
        Dm = aview(OFF_RDM, 2 * 31 * 128, b=2, j=31, q=128)
        m1 = aview(OFF_RC, RC_N + RDM_N, k=8, t=NT)
        ptm = aview(OFF_RC, 4 * 1024, b=4, n=1024)
        h1 = aview(OFF_RC, 16384, f=32, t=512)
        hT_all = stat[:].rearrange("p a n -> p (a n)").bitcast(BF16).rearrange("p (k t) -> p k t", k=8)

        def slot(i):
            return aview(OFF_RW + i * 2048, 2048, k=8, n=256)

        def wff1_off(c):
            return (OFF_RA + c * 4096) if c < 4 else (OFF_RD + (c - 4) * 4096)

        def wff1_blk(c):
            return aview(wff1_off(c), 4096, k=8, n=512)

        def wff2_off(f):
            return (OFF_RB + f * 1024) if f < 16 else (OFF_RW + (f - 16) * 1024)

        def wff2_fc(f):
            return aview(wff2_off(f), 1024)

        WO_SL = 0
        wo_t = aview(OFF_RW, 8192, h=2, k=8, n=512)

        for k in range(8):
            kb.reg(("uT", k, -1), OFF_RA + k * TT, H)
            for b in range(4):
                kb.reg(("uT", k, b), OFF_RA + k * TT + H + b * 512, 512)
            kb.reg(("u", k), OFF_RB + k * TT, TT)
            for b in range(4):
                kb.reg(("z", k, b), OFF_RB + k * NT + b * 512, 512)
            kb.reg(("s", k), OFF_RD + k * NT, NT)
            kb.reg(("m1", k), OFF_RC + k * NT, NT)
            kb.reg(("rw", k), OFF_RW + k * 2048, 2048)
            kb.reg(("wff1", k), wff1_off(k), 4096)
        for b in range(2):
            kb.reg(("sqb", b), OFF_RC + b * 4096, 4096)
            kb.reg(("Dm", b), OFF_RDM + b * 3968, 3968)
            kb.reg(("wo", b), OFF_RW + b * 4096, 4096)
        for b in range(4):
            kb.reg(("ptm", b), OFF_RC + b * 1024, 1024)
        for f in range(32):
            kb.reg(("wff2", f), wff2_off(f), 1024)
            kb.reg(("h1", f), OFF_RC + f * 512, 512)

        st = {"bank": 0, "k": 0, "alt": 0, "xi": 0, "ti": 0}

        def nb():
            b = st["bank"]
            st["bank"] = (b + 1) % 8
            return b

        def PSK(b):
            return ("ps", b)

        def alt():
            st["alt"] ^= 1
            return "act" if st["alt"] else "dve"

        def nti():
            st["ti"] = (st["ti"] + 1) % 4
            return st["ti"]

        def MM(out_ap, lhsT, rhs, start, stop, reads, writes):
            kb.op("pe", lambda e: e.matmul(out_ap, lhsT=lhsT, rhs=rhs, start=start, stop=stop), reads=reads, writes=writes)

        def TR(out_ap, in_ap, ident_ap, reads, writes):
            kb.op("pe", lambda e: e.transpose(out_ap, in_ap, ident_ap), reads=reads, writes=writes)

        def ACT(out_ap, in_ap, func, reads, writes, bias=None, scale=None, accum=None):
            kw = {}
            if bias is not None:
                kw["bias"] = bias
            if scale is not None:
                kw["scale"] = scale
            if accum is not None:
                kw["accum_out"] = accum
            kb.op("act", lambda e: e.activation(out=out_ap, in_=in_ap, func=func, **kw), reads=reads, writes=writes)

        def TTOP(eng, out_ap, in0, in1, op, reads, writes):
            kb.op(eng, lambda e: e.tensor_tensor(out=out_ap, in0=in0, in1=in1, op=op), reads=reads, writes=writes)

        def TS(eng, out_ap, in0, s1, op0, reads, writes):
            kb.op(eng, lambda e: e.tensor_scalar(out=out_ap, in0=in0, scalar1=s1, scalar2=None, op0=op0), reads=reads, writes=writes)

        def STT(out_ap, in0, scalar, in1, op0, op1, reads, writes):
            kb.op("dve", lambda e: e.scalar_tensor_tensor(out=out_ap, in0=in0, scalar=scalar, in1=in1, op0=op0, op1=op1),
                  reads=reads, writes=writes)

        def RECIP(out_ap, in_ap, reads, writes):
            kb.op("dve", lambda e: e.reciprocal(out_ap, in_ap), reads=reads, writes=writes)

        def COPY(eng, out_ap, in_ap, reads, writes):
            if eng == "act":
                ACT(out_ap, in_ap, AF.Copy, reads, writes)
            else:
                kb.op(eng, lambda e: e.tensor_copy(out_ap, in_ap), reads=reads, writes=writes)

        def DMA(eng, out_ap, in_ap, reads, writes, after=()):
            kb.op(eng, lambda e: e.dma_start(out=out_ap, in_=in_ap), reads=reads, writes=writes, dma=True, after=after)

        def wload(sl_ap, src_ap, keys, after=()):
            DMA("pool", sl_ap, src_ap, [], keys, after=after)

        w_in_v = w_in.rearrange("(k p) n -> p k n", p=128)
        wco_v = w_conv_out.rearrange("(k p) n -> p k n", p=128)
        wpo_v = w_pool_out.rearrange("(k p) n -> p k n", p=128)
        wo_v = w_o.rearrange("(k p) n -> p k n", p=128)
        wff1_v = w_ff1.rearrange("(k p) n -> p k n", p=128)
        wff2_v = w_ff2.rearrange("(f p) n -> p f n", p=128)

        DMA("sp", gb[:, 0, :], gbd[0:1, :].partition_broadcast(128), [], [("gb", 0)])
        DMA("sp", identf[:], identd, [], ["identf"])
        xs = aview(OFF_RD, 5 * 2048).bitcast(F32).rearrange("p (b n) -> p b n", b=5)
        for b in range(5):
            kb.reg(("xs", b), OFF_RD + b * 2048, 2048)
        xs_ev = {}

        def s0_stage(j):
            P = 32 if j < 0 else 128
            r0 = 0 if j < 0 else H + 128 * j
            b = 4 if j < 0 else j
            xs_ev[j] = kb.op("sp", lambda e: e.dma_start(out=xs[:P, b, :], in_=xh[r0:r0 + P, :]), writes=[("xs", b)], dma=True)

        for j in (0, 1, 2, 3, -1):
            s0_stage(j)
        DMA("sp", vec[:], vecs, [], ["vec"])
        COPY("dve", identb[:], identf[:], ["identf"], ["identb"])
        kb.op("dve", lambda e: e.memset(onesb[:], 1.0), writes=["onesb"])
        kb.op("dve", lambda e: e.memset(epst[:, 0:1], RMS_EPS), writes=["eps0"])
        kb.op("dve", lambda e: e.memset(epst[:, 1:2], LN_EPS), writes=["eps1"])
        kT_v = vec[:, 32:280].rearrange("p (c j) -> p c j", c=8)

        def vcol(base, cc):
            return vec[:, base + cc:base + cc + 1]

        def rstd_col(P, k, epsc):
            ACT(sm[:P, 16 + k:17 + k], sm[:P, k:k + 1], AF.Sqrt, [("sm", k), "eps%d" % epsc], [("sm", 16 + k)],
                bias=epst[:P, epsc:epsc + 1], scale=1.0 / DM)
            RECIP(sm[:P, 32 + k:33 + k], sm[:P, 16 + k:17 + k], [("sm", 16 + k)], [("sm", 32 + k)])
            return sm[:P, 32 + k:33 + k], ("sm", 32 + k)

        def next_k():
            k = st["k"]
            st["k"] = (k + 1) % 16
            return k

        junk = junkb
        JUNKK = ["junkb"]

        def norm_part(P, src_ap, src_key, gslot):
            k = next_k()
            bi = st["xi"]
            st["xi"] = (bi + 1) % 3
            ACT(junk[:P, :], src_ap, AF.Square, [src_key], JUNKK + [("sm", k)], accum=sm[:P, k:k + 1])
            rs, rsk = rstd_col(P, k, 0)
            STT(xb[:P, bi, :], src_ap, rs, gb[:P, gslot, :], ALU.mult, ALU.mult, [src_key, rsk, ("gb", gslot)], [("xb", bi)])
            return bi

        def transp_part(P, bi, dst_ap, dst_keys):
            b = nb()
            psb = ps[:, b, :].bitcast(BF16)
            for kc in range(8):
                TR(psb[:, kc * 128:kc * 128 + P], xb[:P, bi, kc * 128:(kc + 1) * 128], identb[:P, :P],
                   [("xb", bi), "identb"], [PSK(b)])
            src = psb.rearrange("p (k t) -> p k t", k=8)[:, :, 0:P]
            COPY(alt(), dst_ap, src, [PSK(b)], dst_keys)

        def s0_norm(j):
            P = 32 if j < 0 else 128
            r0 = 0 if j < 0 else H + 128 * j
            if j < 4:
                b = 4 if j < 0 else j
                return norm_part(P, xs[:P, b, :], ("xs", b), 0)
            xi = (j + 1) % 2
            DMA("sp", xt[:P, xi, :], xh[r0:r0 + P, :], [], [("xt", xi)])
            return norm_part(P, xt[:P, xi, :], ("xt", xi), 0)

        def s0_transp(j, bi):
            P = 32 if j < 0 else 128
            r0 = 0 if j < 0 else H + 128 * j
            blk = -1 if j < 0 else j // 4
            transp_part(P, bi, uT[:, :, r0:r0 + P], [("uT", k, blk) for k in range(8)])

        def glu_group(cc, t0, n, blk):
            cp, ccl = cc // 2, cc % 2
            sa, sg, ka, kg = slot(cp), slot(4 + cp), ("rw", cp), ("rw", 4 + cp)
            if n == H:
                bh = nb()
                pa, pg, ra, rg = ps[:, bh, 0:H], ps[:, bh, H:2 * H], PSK(bh), PSK(bh)
            else:
                ba, bg = nb(), nb()
                pa, pg, ra, rg = ps[:, ba, :], ps[:, bg, :], PSK(ba), PSK(bg)
            for (po, sl, key, rk) in ((pa, sa, ka, ra), (pg, sg, kg, rg)):
                for kc in range(8):
                    MM(po, sl[:, kc, ccl * 128:(ccl + 1) * 128], uT[:, kc, t0:t0 + n], kc == 0, kc == 7,
                       [("uT", kc, blk), key], [rk])
            t_i = nti()
            ACT(tmpA[:, t_i, 0:n], pg, AF.Sigmoid, [rg], [("tmpA", t_i)])
            TTOP("dve", u[:, cc, t0:t0 + n], pa, tmpA[:, t_i, 0:n], ALU.mult, [ra, ("tmpA", t_i)], [("u", cc)])

        for cp in range(4):
            aft = [xs_ev[1]] if cp == 0 else [xs_ev[-1]]
            wload(slot(cp), w_in_v[:, :, cp * 256:(cp + 1) * 256], [("rw", cp)], after=aft)
            wload(slot(4 + cp), w_in_v[:, :, 1024 + cp * 256:1024 + (cp + 1) * 256], [("rw", 4 + cp)], after=aft)
        DMA("pool", pmb[:].rearrange("p g n -> p (g n)"), pmd, [], ["pmb"])
        pend = None
        for j in (0, 1, 2, 3, -1):
            bi_c = s0_norm(j)
            if pend is not None:
                s0_transp(*pend)
            pend = (j, bi_c)
        s0_transp(*pend)
        for blk in range(4):
            for cc in range(8):
                nxt = 4 * (blk + 1) + cc // 2
                if cc % 2 == 0 and nxt < 16:
                    bi_n = s0_norm(nxt)
                glu_group(cc, H + blk * 512, 512, blk)
                if cc % 2 == 1 and nxt < 16:
                    s0_transp(nxt, bi_n)
            if blk == 0:
                for cc in range(8):
                    glu_group(cc, 0, H, -1)

        def gate_load(cp, base, wmat_v, gcol0):
            wload(slot(base), wmat_v[:, :, cp * 256:(cp + 1) * 256], [("rw", base)])
            wload(slot(base + 1), w_in_v[:, :, gcol0 + cp * 256:gcol0 + (cp + 1) * 256], [("rw", base + 1)])

        CB = OFF_RC
        utm = [aview(CB + b * 2048, 2048, g=16, r=16, c=8) for b in range(2)]
        uhtm = aview(CB + 4096, 2048, g=16, r=16, c=8)
        UTt = [aview(CB + 6144 + b * 2080, 2080, g=16, m=130) for b in range(2)]
        cvtm = [aview(CB + 10304 + b * 2048, 2048, r=16, c=128) for b in range(2)]
        Wc = [aview(OFF_RW + b * 6144, 6144, g=16, d=3, n=128) for b in range(2)]
        for b in range(2):
            kb.reg(("utm", b), CB + b * 2048, 2048)
            kb.reg(("UT", b), CB + 6144 + b * 2080, 2080)
            kb.reg(("cvtm", b), CB + 10304 + b * 2048, 2048)
            kb.reg(("Wc", b), OFF_RW + b * 6144, 6144)
        kb.reg("uhtm", CB + 4096, 2048)

        def pbank():
            b = nb()
            return b, ps[:, b, :].bitcast(BF16)

        cva = {"i": 0}

        def calt():
            cva["i"] = (cva["i"] + 1) % 3
            return "act" if cva["i"] == 0 else "dve"

        def cv_load(cc):
            DMA("pool", Wc[cc % 2].rearrange("p g d n -> p (g d n)"), convw[cc], [], [("Wc", cc % 2)])

        def cv_t1(cc):
            bf = cc % 2
            um = u[:, cc, H:H + NT].rearrange("p (m r) -> p m r", r=16)
            uh = u[:, cc, 0:H].rearrange("p (m r) -> p m r", r=16)
            for half in range(2):
                b, pb = pbank()
                for r8 in range(8):
                    r = half * 8 + r8
                    TR(pb[:, r8 * 128:(r8 + 1) * 128], um[:, :, r], identb[:], [("u", cc), "identb"], [PSK(b)])
                COPY(calt(), utm[bf][:, :, half * 8:half * 8 + 8, :].rearrange("p g r c -> p r g c"),
                     pb.rearrange("p (r g c) -> p r g c", r=8, g=16), [PSK(b)], [("utm", bf)])
            for half in range(2):
                b, pb = pbank()
                for r8 in range(8):
                    r = half * 8 + r8
                    TR(pb[:2, r8 * 128:(r8 + 1) * 128], uh[:, :, r], identb[:], [("u", cc), "identb"], [PSK(b)])
                COPY(calt(), uhtm[:2, :, half * 8:half * 8 + 8, :].rearrange("p g r c -> p r g c"),
                     pb[:2, :].rearrange("p (r g c) -> p r g c", r=8, g=16), [PSK(b)], ["uhtm"])

        def cv_t2(cc):
            bf = cc % 2
            for half in range(2):
                b, pb = pbank()
                for g8 in range(8):
                    gl = half * 8 + g8
                    TR(pb[:, g8 * 128:(g8 + 1) * 128], utm[bf][:, gl, :, :].rearrange("p r c -> p (r c)"), identb[:],
                       [("utm", bf), "identb"], [PSK(b)])
                COPY(calt(), UTt[bf][:, half * 8:half * 8 + 8, 2:130], pb.rearrange("p (g m) -> p g m", g=8), [PSK(b)], [("UT", bf)])
            b, pb = pbank()
            for gl in range(16):
                TR(pb[:, gl * 2:gl * 2 + 2], uhtm[:2, gl, :, :].rearrange("p r c -> p (r c)"), identb[:2, :2], ["uhtm", "identb"], [PSK(b)])
            COPY(calt(), UTt[bf][:, :, 0:2], pb[:, 0:32].rearrange("p (g m) -> p g m", g=16), [PSK(b)], [("UT", bf)])

        def cv_mm(cc):
            bf = cc % 2
            for q in range(4):
                b = nb()
                for g4 in range(4):
                    gl = q * 4 + g4
                    for d in range(3):
                        MM(ps[:, b, g4 * 128:(g4 + 1) * 128], UTt[bf][:, gl, d:d + 128], Wc[bf][:, gl, d, :], d == 0, d == 2,
                           [("UT", bf), ("Wc", bf)], [PSK(b)])
                src = ps[:, b, :].rearrange("p (g r c) -> p g r c", g=4, r=16)
                dst = cvtm[bf][:, :, q * 32:(q + 1) * 32].rearrange("p r (g c) -> p g r c", g=4)
                COPY(calt(), dst, src, [PSK(b)], [("cvtm", bf)])

        def cv_t3(cc):
            bf = cc % 2
            sv = s_t[:, cc, :].rearrange("p (m r) -> p m r", r=16)
            for half in range(2):
                b, pb = pbank()
                for r8 in range(8):
                    r = half * 8 + r8
                    TR(pb[:, r8 * 128:(r8 + 1) * 128], cvtm[bf][:, r, :], identb[:], [("cvtm", bf), "identb"], [PSK(b)])
                if half == 0:
                    ACT(sv[:, :, 0:8], pb.rearrange("p (r m) -> p m r", r=8), AF.Identity, [PSK(b), "vec"], [("s", cc)],
                        bias=vcol(0, cc))
                else:
                    TS("dve", sv[:, :, 8:16], pb.rearrange("p (r m) -> p m r", r=8), vcol(0, cc), ALU.add, [PSK(b), "vec"], [("s", cc)])

        cv_load(0)
        cv_load(1)
        for step in range(8 + 3):
            if 0 <= step - 1 < 8:
                cv_t2(step - 1)
            if step < 8:
                cv_t1(step)
            if 0 <= step - 3 < 8:
                cv_t3(step - 3)
            if 0 <= step - 2 < 8:
                cv_mm(step - 2)
                if step - 2 + 2 < 8:
                    cv_load(step - 2 + 2)

        if dbg == "cv":
            DMA("sp", dbg_cv, s_t, [("s", k) for k in range(8)], ["dbg_cv"])
            kb.final_wait("sp", ["dbg_cv"])
            sems = {}
            for e in ("pe", "act", "dve", "pool"):
                sems[e] = es.enter_context(nc.semaphore("s_" + e))
            for i in range(KB.N_DMA_SEM):
                sems[("d", i)] = es.enter_context(nc.semaphore("s_d%d" % i))
            for i in range(kb.n_sw):
                sems[("w", i)] = es.enter_context(nc.semaphore("s_w%d" % i))
            block = es.enter_context(nc.Block())
            kb.emit(sems, block)
            return nc

        for q in range(4):
            wload(slot(4 + q), w_in_v[:, :, 2048 + q * 256:2048 + (q + 1) * 256], [("rw", 4 + q)])
        gate_load(0, 0, wpo_v, 4096)
        gate_load(1, 2, wpo_v, 4096)

        for cc in range(8):
            sb = cc % 2
            if cc % 4 == 1:
                ACT(sqb[:, sb, 0:NT], s_t[:, cc, :], AF.Square, [("s", cc)], [("sqb", sb)])
            else:
                TTOP("dve", sqb[:, sb, 0:NT], s_t[:, cc, :], s_t[:, cc, :], ALU.mult, [("s", cc)], [("sqb", sb)])
            for blk in range(4):
                MM(ps[:, blk, :], onesb[:], s_t[:, cc, blk * 512:(blk + 1) * 512], cc == 0, cc == 7,
                   ["onesb", ("s", cc)], [PSK(blk)])
                MM(ps[:, 4 + blk, :], onesb[:], sqb[:, sb, blk * 512:(blk + 1) * 512], cc == 0, cc == 7,
                   ["onesb", ("sqb", sb)], [PSK(4 + blk)])
        st["bank"] = 0
        gbv = gb[:].rearrange("p a (b n) -> p (a b) n", b=2)

        def ln_stats(blk):
            mean, var = stat[:, blk, :], gbv[:, blk, :]
            t_i = nti()
            TS("dve", mean, ps[:, blk, :], 1.0 / DM, ALU.mult, [PSK(blk)], [("stat", blk)])
            TTOP("dve", tmpA[:, t_i, :], mean, mean, ALU.mult, [("stat", blk)], [("tmpA", t_i)])
            STT(var, ps[:, 4 + blk, :], 1.0 / DM, tmpA[:, t_i, :], ALU.mult, ALU.subtract, [PSK(4 + blk), ("tmpA", t_i)], [("gb", blk // 2)])

        def ln_stats_finish(blk):
            var = gbv[:, blk, :]
            ACT(var, var, AF.Sqrt, [("gb", blk // 2), "eps1"], [("gb", blk // 2)], bias=epst[:, 1:2], scale=1.0)
            RECIP(var, var, [("gb", blk // 2)], [("gb", blk // 2)])

        def ln_unit(blk, cc):
            mean, var = stat[:, blk, :], gbv[:, blk, :]
            t_i = nti()
            sl = s_t[:, cc, blk * 512:(blk + 1) * 512]
            TTOP("dve", tmpA[:, t_i, :], sl, mean, ALU.subtract, [("s", cc), ("stat", blk)], [("tmpA", t_i)])
            TTOP("dve", tmpA[:, t_i, :], tmpA[:, t_i, :], var, ALU.mult, [("tmpA", t_i), ("gb", blk // 2)], [("tmpA", t_i)])
            ACT(sl, tmpA[:, t_i, :], AF.Silu, [("tmpA", t_i), "vec"], [("s", cc)], bias=vcol(16, cc), scale=vcol(8, cc))

        for blk in range(4):
            ln_stats(blk)
        ln_stats_finish(0)

        def s4a_proj(j):
            P = 32 if j < 0 else 128
            t0 = 0 if j < 0 else H + 128 * j
            blk = -1 if j < 0 else j // 4
            pb = 3 if j < 0 else j % 3
            bq = (nb(), nb())
            for q in range(4):
                po = ps[:P, bq[q // 2], (q % 2) * 256:(q % 2) * 256 + 256]
                for kc in range(8):
                    MM(po, uT[:, kc, t0:t0 + P], slot(4 + q)[:, kc, :], kc == 0, kc == 7, [("uT", kc, blk), ("rw", 4 + q)],
                       [PSK(bq[q // 2])])
            for hh in range(2):
                COPY(alt(), ptm[:P, pb, hh * 512:(hh + 1) * 512], ps[:P, bq[hh], :], [PSK(bq[hh])], [("ptm", pb)])

        def s4a_pool(j):
            pb = j % 3
            bz = (nb(), nb())
            ppb = 3 if j == 0 else (j - 1) % 3
            PP = 32 if j == 0 else 128
            for cc in range(8):
                g = cc // 2
                zo = ps[:, bz[cc // 4], (cc % 4) * 128:(cc % 4) * 128 + 128]
                acol = 0 if j == 0 else 128
                MM(zo, ptm[:, pb, cc * 128:(cc + 1) * 128], pmb[:, g, acol:acol + 128], True, False,
                   [("ptm", pb), "pmb"], [PSK(bz[cc // 4])])
                pcol = 272 if j == 0 else 256
                MM(zo[:, 0:16], ptm[:PP, ppb, cc * 128:(cc + 1) * 128], pmb[:PP, g, pcol:pcol + 16], False, True,
                   [("ptm", ppb), "pmb"], [PSK(bz[cc // 4])])
            for hh in range(2):
                COPY(alt(), z_t[:, 4 * hh:4 * hh + 4, j * 128:(j + 1) * 128],
                     ps[:, bz[hh], :].rearrange("p (c t) -> p c t", c=4), [PSK(bz[hh])], [("z", 4 * hh + c, j // 4) for c in range(4)])

        s4a_proj(-1)
        s4a_proj(0)
        for blk in range(4):
            if blk + 1 < 4:
                ln_stats_finish(blk + 1)
            for tl in range(4):
                j = 4 * blk + tl
                ln_unit(blk, 2 * tl)
                if j + 1 < 16:
                    s4a_proj(j + 1)
                ln_unit(blk, 2 * tl + 1)
                s4a_pool(j)

        pw_v = slot(4).rearrange("p k n -> p (k n)").rearrange("p (g k n) -> p g k n", g=4, k=2)
        wload(pw_v, pool_w.rearrange("g (k p) n -> p g k n", p=128), [("rw", 4)])
        gate_load(0, 6, wco_v, 3072)
        for g in range(4):
            for blk in range(4):
                bo = (nb(), nb())
                for oc in range(2):
                    for k2 in range(2):
                        MM(ps[:, bo[oc], :], pw_v[:, g, k2, oc * 128:(oc + 1) * 128], z_t[:, 2 * g + k2, blk * 512:(blk + 1) * 512],
                           k2 == 0, k2 == 1, [("rw", 4), ("z", 2 * g + k2, blk)], [PSK(bo[oc])])
                ACT(z_t[:, 2 * g, blk * 512:(blk + 1) * 512], ps[:, bo[0], :], AF.Identity, [PSK(bo[0]), "vec"],
                    [("z", 2 * g, blk)], scale=vcol(24, 2 * g))
                TS("dve", z_t[:, 2 * g + 1, blk * 512:(blk + 1) * 512], ps[:, bo[1], :], vcol(24, 2 * g + 1), ALU.mult,
                   [PSK(bo[1]), "vec"], [("z", 2 * g + 1, blk)])

        def gated_group(cc, ccl, base, src, srck, blk, first):
            sw, sgt, kw, kg = slot(base), slot(base + 1), ("rw", base), ("rw", base + 1)
            by, bg = nb(), nb()
            for kc in range(8):
                MM(ps[:, by, :], sw[:, kc, ccl * 128:(ccl + 1) * 128], src[:, kc, blk * 512:(blk + 1) * 512], kc == 0, kc == 7,
                   [(srck, kc, blk) if srck == "z" else (srck, kc), kw], [PSK(by)])
            for kc in range(8):
                MM(ps[:, bg, :], sgt[:, kc, ccl * 128:(ccl + 1) * 128], uT[:, kc, H + blk * 512:H + (blk + 1) * 512], kc == 0, kc == 7,
                   [("uT", kc, blk), kg], [PSK(bg)])
            t_i = nti()
            ACT(tmpA[:, t_i, :], ps[:, bg, :], AF.Sigmoid, [PSK(bg)], [("tmpA", t_i)])
            dst = m1[:, cc, blk * 512:(blk + 1) * 512]
            if first:
                TTOP("dve", dst, ps[:, by, :], tmpA[:, t_i, :], ALU.mult, [PSK(by), ("tmpA", t_i)], [("m1", cc)])
            else:
                TTOP("dve", tmpA[:, t_i, :], ps[:, by, :], tmpA[:, t_i, :], ALU.mult, [PSK(by), ("tmpA", t_i)], [("tmpA", t_i)])
                TTOP("pool" if (blk + cc) % 2 else "dve", dst, tmpA[:, t_i, :], dst, ALU.add, [("tmpA", t_i), ("m1", cc)], [("m1", cc)])

        for cp in range(4):
            base = (cp % 2) * 2
            for ccl in range(2):
                for blk in range(4):
                    gated_group(2 * cp + ccl, ccl, base, z_t, "z", blk, True)
            if cp + 2 < 4:
                gate_load(cp + 2, base, wpo_v, 4096)
            if cp == 0:
                gate_load(1, 4, wco_v, 3072)
        for c in range(4):
            wload(aview(OFF_RB + c * 4096, 4096, f=4, n=1024), wff2_v[:, 4 * c:4 * c + 4, :], [("wff2", f) for f in range(4 * c, 4 * c + 4)])
        for cp in range(4):
            base = 6 - (cp % 2) * 2
            for ccl in range(2):
                for blk in range(4):
                    gated_group(2 * cp + ccl, ccl, base, s_t, "s", blk, False)
            if cp + 2 < 4:
                gate_load(cp + 2, base, wco_v, 3072)
            if cp == 1:
                for hh in range(2):
                    wload(wo_t[:, hh, :, :], wo_v[:, :, hh * 512:(hh + 1) * 512], [("wo", hh)])

        def ff1_load(c):
            wload(wff1_blk(c), wff1_v[:, :, c * 512:(c + 1) * 512], [("wff1", c)])

        def nb2():
            return (nb(), nb())

        def resid_a(b2, gslot, tb):
            k = next_k()
            k2 = next_k()
            tt = tmpA[:, 2 * tb:2 * tb + 2, :].rearrange("p a n -> p (a n)")
            kk = [("tmpA", 2 * tb), ("tmpA", 2 * tb + 1)]
            def tt_half(hh):
                TTOP("dve", tt[:, hh * 512:(hh + 1) * 512], ps[:, b2[hh], :], gb[:, gslot, hh * 512:(hh + 1) * 512], ALU.mult,
                     [PSK(b2[hh]), ("gb", gslot)], [("tmpA", 2 * tb + hh)])

            tt_half(0)
            ACT(junk[:, 512:1024], ps[:, b2[1], :], AF.Square, [PSK(b2[1])], JUNKK + [("sm", k2)], accum=sm[:, k2:k2 + 1])
            tt_half(1)
            ACT(junk[:, 0:512], ps[:, b2[0], :], AF.Square, [PSK(b2[0])], JUNKK + [("sm", k)], accum=sm[:, k:k + 1])
            TTOP("dve", sm[:, k:k + 1], sm[:, k:k + 1], sm[:, k2:k2 + 1], ALU.add, [("sm", k), ("sm", k2)], [("sm", k)])
            ACT(sm[:, 16 + k:17 + k], sm[:, k:k + 1], AF.Sqrt, [("sm", k), "eps0"], [("sm", 16 + k)],
                bias=epst[:, 0:1], scale=1.0 / DM)
            return (k, tt, kk)

        def resid_b(ctx, res_ap, res_key, dst_rows, outk):
            k, tt, kk = ctx
            RECIP(sm[:, 32 + k:33 + k], sm[:, 16 + k:17 + k], [("sm", 16 + k)], [("sm", 32 + k)])
            STT(res_ap, tt, sm[:, 32 + k:33 + k], res_ap, ALU.mult, ALU.add, kk + [("sm", 32 + k), res_key], [res_key])
            DMA("sp", out[dst_rows:dst_rows + 128, :], res_ap, [res_key], [outk])

        DMA("sp", gb[:, 1, :], gbd[1:2, :].partition_broadcast(128), [], [("gb", 1)])
        DMA("sp", gb[:, 0, :], gbd[2:3, :].partition_broadcast(128), [], [("gb", 0)])
        for c in range(7):
            ff1_load(c)
        xr = aview(OFF_RD + 12288, 4096).bitcast(F32).rearrange("p (b n) -> p b n", b=2)
        kb.reg(("xt", 2), OFF_RD + 12288, 2048)
        kb.reg(("xt", 3), OFF_RD + 14336, 2048)

        def xbuf(i):
            return xt[:, i, :] if i < 2 else xr[:, i - 2, :]

        def s6_transp(j, bi):
            tl = j % 4
            transp_part(128, bi, hT_all[:, :, tl * 128:(tl + 1) * 128], [("hT", tl)])

        def s5_a(j):
            bi = j % 4
            DMA("sp", xbuf(bi), xh[H + 128 * j:H + 128 * (j + 1), :], [], [("xt", bi)])
            b2 = nb2()
            for hh in range(2):
                for kc in range(8):
                    MM(ps[:, b2[hh], :], m1[:, kc, j * 128:(j + 1) * 128], wo_t[:, hh, kc, :], kc == 0, kc == 7,
                       [("m1", kc), ("wo", hh)], [PSK(b2[hh])])
            return resid_a(b2, 1, j % 2)

        def s5_b(j, ctx):
            bi = j % 4
            resid_b(ctx, xbuf(bi), ("xt", bi), 128 * j, ("outh", j))
            if j < 4:
                s5_pend.append((j, norm_part(128, xbuf(bi), ("xt", bi), 0)))
            if s5_pend and s5_pend[0][0] <= j - 2:
                s6_transp(*s5_pend.pop(0))

        s5_pend = []
        pend = None
        for j in range(16):
            ctx = s5_a(j)
            if pend is not None:
                s5_b(*pend)
            pend = (j, ctx)
        s5_b(*pend)
        ff1_load(7)
        for c in range(4, 8):
            wload(aview(OFF_RW + (c - 4) * 4096, 4096, f=4, n=1024), wff2_v[:, 4 * c:4 * c + 4, :],
                  [("wff2", f) for f in range(4 * c, 4 * c + 4)])

        DMA("sp", gb[:, 1, :], gbd[3:4, :].partition_broadcast(128), [], [("gb", 1)])

        def s6_norm(j):
            DMA("sp", xt[:, 0, :], out[128 * j:128 * (j + 1), :], [("outh", j)], [("xt", 0)])
            return norm_part(128, xt[:, 0, :], ("xt", 0), 0)

        for blk in range(4):
            for fc in range(32):
                b = nb()
                c, fl = fc // 4, fc % 4
                wv = wff1_blk(c)
                for kc in range(8):
                    MM(ps[:, b, :], wv[:, kc, fl * 128:(fl + 1) * 128], hT_all[:, kc, :], kc == 0, kc == 7,
                       [("wff1", c)] + [("hT", t) for t in range(4)], [PSK(b)])
                t_i = nti() % 2
                ACT(tmpA[:, t_i, :], ps[:, b, :], AF.Relu, [PSK(b)], [("tmpA", t_i)])
                sq_eng = "dve"
                TTOP(sq_eng, h1[:, fc, :], tmpA[:, t_i, :], tmpA[:, t_i, :], ALU.mult, [("tmpA", t_i)], [("h1", fc)])
            pend = None
            for tl in range(4):
                j = 4 * blk + tl
                nj = j + 4
                if nj < 16:
                    bi_n = s6_norm(nj)
                DMA("sp", xt[:, 1, :], out[128 * j:128 * (j + 1), :], [("outh", j)], [("xt", 1)])
                b2 = nb2()
                for hh in range(2):
                    for fc in range(32):
                        MM(ps[:, b2[hh], :], h1[:, fc, tl * 128:(tl + 1) * 128], wff2_fc(fc)[:, hh * 512:(hh + 1) * 512],
                           fc == 0, fc == 31, [("h1", fc), ("wff2", fc)], [PSK(b2[hh])])
                if nj < 16:
                    s6_transp(nj, bi_n)
                ctx = resid_a(b2, 1, 1)
                resid_b(ctx, xt[:, 1, :], ("xt", 1), 128 * j, ("outh", j))

        kb.final_wait("sp", [("outh", j) for j in range(16)])
        sems = {}
        for e in ("pe", "act", "dve", "pool"):
            sems[e] = es.enter_context(nc.semaphore("s_" + e))
        for i in range(KB.N_DMA_SEM):
            sems[("d", i)] = es.enter_context(nc.semaphore("s_d%d" % i))
        for i in range(kb.n_sw):
            sems[("w", i)] = es.enter_context(nc.semaphore("s_w%d" % i))
        block = es.enter_context(nc.Block())
        kb.emit(sems, block)
    return nc


def _pool_mats(first):
    pm = np.zeros((128, 4, 288), np.float32)
    t = np.arange(128)
    for g, w in enumerate(POOL_W):
        d = t[None, :] - t[:, None]
        band = ((d >= 0) & (d < w)).astype(np.float32)
        std = band / w - np.eye(128, dtype=np.float32)
        cnt = np.minimum(t + 1, w).astype(np.float32)
        fst = band / cnt[None, :] - np.eye(128, dtype=np.float32)
        pm[:, g, 0:128] = fst if first else std
        pm[:, g, 128:256] = std
        tp = np.arange(128)[:, None]
        tt = np.arange(16)[None, :]
        prev = ((tt + 128 - tp) < w).astype(np.float32) / w
        pm[:, g, 256:272] = prev
        pm[0:32, g, 272:288] = prev[96:128, :]
    return pm.reshape(128, 4 * 288)


def _conv_mats(dw_kernel):
    k = np.asarray(dw_kernel, np.float32)
    W = np.zeros((8, 16, 8, 16, 3, 16, 8), np.float32)
    kk = k.reshape(31, 8, 16, 8)
    for d in range(3):
        for r in range(16):
            for rp in range(16):
                j = 16 * (d - 2) + r - rp + 30
                if 0 <= j <= 30:
                    for c in range(8):
                        W[:, r, c, :, d, rp, c] = kk[j, :, :, c]
    return W.reshape(8, 128, 16 * 3 * 128)


def _prep_inputs(inp):
    f = lambda a: np.ascontiguousarray(np.asarray(a, dtype=np.float32))
    x = f(inp["x"])
    cm = lambda v: f(v).reshape(8, 128).T
    vecs = np.zeros((128, 280), np.float32)
    vecs[:, 0:8] = cm(inp["dw_bias"])
    vecs[:, 8:16] = cm(inp["conv_ln_g"])
    vecs[:, 16:24] = cm(inp["conv_ln_b"])
    vecs[:, 24:32] = cm(inp["pool_scale"])
    kT = f(inp["dw_kernel"]).T.reshape(8, 128, 31).transpose(1, 0, 2)
    vecs[:, 32:280] = kT.reshape(128, 248)
    gbm = np.stack([f(inp["mix_pre_g"]), f(inp["mix_post_g"]), f(inp["mlp_pre_g"]), f(inp["mlp_post_g"])])
    shared = {k: f(inp[k]) for k in ("w_in", "w_conv_out", "pool_w", "w_pool_out", "w_o", "w_ff1", "w_ff2")}
    shared["vecs"] = vecs
    shared["gb"] = gbm
    shared["ident"] = np.eye(128, dtype=np.float32)
    shared["convw"] = _conv_mats(inp["dw_kernel"])
    maps = []
    for c in range(NCORES):
        b, half = divmod(c, 2)
        xh = np.zeros((TT, DM), np.float32)
        if half == 1:
            xh[:, :] = x[b, NT - H:2 * NT, :]
        else:
            xh[H:, :] = x[b, 0:NT, :]
        m = dict(shared)
        m["xh"] = xh
        m["pm"] = _pool_mats(first=(half == 0))
        maps.append(m)
    return maps


def kernel(**inputs):
    maps = _prep_inputs(inputs)
    nc = build_program()
    res = run_bass_kernel_spmd(nc, maps, core_ids=list(range(NCORES)))
    outs = [np.asarray(r["out"], dtype=np.float32) for r in res.results]
    y = np.stack(outs).reshape(4, 2 * NT, DM)
    return y
```
